# Optimizing a Trainium2 kernel written in Bass

```python
import math
import jax, jax.numpy as jnp
from jax import lax
import numpy as np

D_MODEL = 1024
BATCH = 4
SEQ = 8192
DEPTH = 2

PLE_DIM = 256
MIX_WIDTH = D_MODEL
RET_HEAD_DIM = 128
RET_HEADS = (MIX_WIDTH // 2) // RET_HEAD_DIM
RET_WIDTH = RET_HEADS * RET_HEAD_DIM
MOBA_HEAD_DIM = 128
MOBA_HEADS = (MIX_WIDTH - RET_WIDTH) // MOBA_HEAD_DIM
MOBA_WIDTH = MOBA_HEADS * MOBA_HEAD_DIM
SPLIT_SIZES = [RET_WIDTH] * 4 + [MOBA_WIDTH] * 4
IN_WIDTH = sum(SPLIT_SIZES)
RET_CHUNK = 128
MOBA_BLOCK = 256
MOBA_TOPK = 3
MOBA_Q_CHUNK = 64
N_BUCKETS = 32
MAX_DISTANCE = 2048
ROPE_THETA = 10000.0
EPS = 1e-6
NEG_INF = -1e30

kernel_name = "hymba_retention_moba_hybrid"


def rms_norm(x, g=None):
    xf = x.astype(jnp.float32)
    y = xf * lax.rsqrt(jnp.mean(xf * xf, axis=-1, keepdims=True) + EPS)
    if g is not None:
        y = y * g.astype(jnp.float32)
    return y.astype(x.dtype)


def rotary(x, pos):
    half = x.shape[-1] // 2
    inv = ROPE_THETA ** (-jnp.arange(half, dtype=jnp.float32) / half)
    ang = pos.astype(jnp.float32)[:, None] * inv[None, :]
    cos = jnp.cos(ang)[None, :, None, :]
    sin = jnp.sin(ang)[None, :, None, :]
    x1, x2 = x[..., :half], x[..., half:]
    return jnp.concatenate([x1 * cos - x2 * sin, x1 * sin + x2 * cos], axis=-1)


def t5_bucket(n):
    max_exact = N_BUCKETS // 2
    nf = jnp.maximum(n, 1).astype(jnp.float32)
    large = max_exact + (jnp.log(nf / max_exact) / math.log(MAX_DISTANCE / max_exact)
                         * (N_BUCKETS - max_exact)).astype(jnp.int32)
    large = jnp.minimum(large, N_BUCKETS - 1)
    return jnp.where(n < max_exact, n, large)


def retention(q, k, v, gain):
    B, S, H, d = q.shape
    C = RET_CHUNK
    nc = S // C
    log_g = jnp.log1p(-jnp.power(2.0, -5.0 - jnp.arange(H, dtype=jnp.float32)))

    def chunks(t):
        return t.reshape(B, nc, C, H, d).transpose(0, 3, 1, 2, 4)

    qc, kc, vc = chunks(q), chunks(k * (d ** -0.5)), chunks(v)
    j = jnp.arange(C, dtype=jnp.float32)
    rel = j[:, None] - j[None, :]
    decay = jnp.where(rel >= 0, jnp.exp(log_g[:, None, None] * jnp.maximum(rel, 0.0)[None]), 0.0)
    s = jnp.einsum('bhcnd,bhcmd->bhcnm', qc, kc) * decay[None, :, None]
    inner = jnp.einsum('bhcnm,bhcme->bhcne', s, vc)
    k_dec = kc * jnp.exp(log_g[:, None] * (C - 1 - j)[None])[None, :, None, :, None]
    kv = jnp.einsum('bhcmd,bhcme->cbhde', k_dec, vc)
    chunk_decay = jnp.exp(log_g * C)[None, :, None, None]

    def step(state, kv_c):
        return state * chunk_decay + kv_c, state

    _, prev = lax.scan(step, jnp.zeros((B, H, d, d), jnp.float32), kv)
    q_dec = qc * jnp.exp(log_g[:, None] * (j + 1.0)[None])[None, :, None, :, None]
    cross = jnp.einsum('bhcnd,cbhde->bhcne', q_dec, prev)
    o = (inner + cross).transpose(0, 2, 3, 1, 4).reshape(B, S, H, d)
    mu = jnp.mean(o, axis=-1, keepdims=True)
    var = jnp.mean(jnp.square(o - mu), axis=-1, keepdims=True)
    o = (o - mu) * lax.rsqrt(var + EPS)
    return o.reshape(B, S, H * d) * gain.astype(jnp.float32)


def moba(q, k, v, rel_bias):
    B, S, H, d = q.shape
    BLK, QC = MOBA_BLOCK, MOBA_Q_CHUNK
    nb = -(-S // BLK)
    pad = nb * BLK - S
    q = q.transpose(0, 2, 1, 3) * (d ** -0.5)
    kp = jnp.pad(k.transpose(0, 2, 1, 3), ((0, 0), (0, 0), (0, pad), (0, 0)))
    vp = jnp.pad(v.transpose(0, 2, 1, 3), ((0, 0), (0, 0), (0, pad), (0, 0)))
    kb = kp.reshape(B, H, nb, BLK, d)
    vb = vp.reshape(B, H, nb, BLK, d)
    kmean = jnp.mean(kb, axis=3)

    qpos = jnp.arange(S)
    qblock = qpos // BLK
    gate = jnp.einsum('bhsd,bhnd->bhsn', q, kmean)
    past = jnp.arange(nb)[None, :] < qblock[:, None]
    gate = jnp.where(past[None, None], gate, NEG_INF)
    n_sel = min(MOBA_TOPK, nb)
    _, top_idx = lax.top_k(gate, n_sel)

    nq = S // QC
    qx = q.reshape(B, H, nq, QC, d).transpose(2, 0, 1, 3, 4)
    ix = top_idx.reshape(B, H, nq, QC, n_sel).transpose(2, 0, 1, 3, 4)
    bi = jnp.arange(B)[:, None, None, None]
    hi = jnp.arange(H)[None, :, None, None]
    offs = jnp.arange(BLK)
    bias_h = rel_bias.astype(jnp.float32).T

    def one_chunk(args):
        c, qc, ic = args
        qp = c * QC + jnp.arange(QC)
        qblk = (c * QC) // BLK
        ksel = kb[bi, hi, ic]
        vsel = vb[bi, hi, ic]
        kpos = ic[..., None] * BLK + offs
        dist = qp[None, None, :, None, None] - kpos
        ok = (ic < qblk)[..., None]
        s_sel = (jnp.einsum('bhqd,bhqnkd->bhqnk', qc, ksel)
                 + bias_h[hi[..., None], t5_bucket(jnp.maximum(dist, 0))])
        s_sel = jnp.where(ok, s_sel, NEG_INF).reshape(B, H, QC, n_sel * BLK)
        kown = lax.dynamic_index_in_dim(kb, qblk, axis=2, keepdims=False)
        vown = lax.dynamic_index_in_dim(vb, qblk, axis=2, keepdims=False)
        dist_own = qp[:, None] - (qblk * BLK + offs)[None, :]
        s_own = (jnp.einsum('bhqd,bhkd->bhqk', qc, kown)
                 + bias_h[:, t5_bucket(jnp.maximum(dist_own, 0))][None])
        s_own = jnp.where(dist_own >= 0, s_own, NEG_INF)
        pr = jax.nn.softmax(jnp.concatenate([s_sel, s_own], axis=-1), axis=-1)
        p_sel = pr[..., :n_sel * BLK].reshape(B, H, QC, n_sel, BLK)
        p_own = pr[..., n_sel * BLK:]
        return (jnp.einsum('bhqnk,bhqnkd->bhqd', p_sel, vsel)
                + jnp.einsum('bhqk,bhkd->bhqd', p_own, vown))

    out = lax.map(one_chunk, (jnp.arange(nq), qx, ix))
    return out.transpose(1, 0, 3, 2, 4).reshape(B, S, H * d)


def setup_inputs(seed: int = 0) -> dict:
    key = jax.random.key(seed)
    ks = jax.random.split(key, 12)
    f = jnp.float32
    nrm = jax.random.normal
    return {
        "x": nrm(ks[0], (BATCH, SEQ, D_MODEL), f),
        "p": nrm(ks[1], (DEPTH, BATCH, SEQ, PLE_DIM), f),
        "norm_g": 1.0 + 0.02 * nrm(ks[2], (DEPTH, D_MODEL), f),
        "w_in": nrm(ks[3], (DEPTH, D_MODEL, IN_WIDTH), f) * D_MODEL ** -0.5,
        "ret_norm_g": 1.0 + 0.02 * nrm(ks[4], (DEPTH, RET_WIDTH), f),
        "q_norm_g": 1.0 + 0.02 * nrm(ks[5], (DEPTH, MOBA_HEAD_DIM), f),
        "k_norm_g": 1.0 + 0.02 * nrm(ks[6], (DEPTH, MOBA_HEAD_DIM), f),
        "rel_bias": 0.5 * nrm(ks[7], (N_BUCKETS, MOBA_HEADS), f),
        "w_out": nrm(ks[8], (DEPTH, MIX_WIDTH, D_MODEL), f) * MIX_WIDTH ** -0.5,
        "w_ple": nrm(ks[9], (DEPTH, PLE_DIM, D_MODEL), f) * PLE_DIM ** -0.5,
        "ple_norm_g": 1.0 + 0.02 * nrm(ks[10], (DEPTH, D_MODEL), f),
        "w_ple_gate": nrm(ks[11], (DEPTH, D_MODEL, D_MODEL), f) * D_MODEL ** -0.5,
    }


def reference(x, p, norm_g, w_in, ret_norm_g, q_norm_g, k_norm_g, rel_bias,
              w_out, w_ple, ple_norm_g, w_ple_gate):
    B, S, _ = x.shape
    f32 = jnp.float32
    pos = jnp.arange(S)
    cuts = list(np.cumsum(SPLIT_SIZES)[:-1])

    def heads(t, H):
        return t.astype(f32).reshape(B, S, H, -1)

    for i in range(DEPTH):
        h = rms_norm(x, norm_g[i])
        proj = h @ w_in[i]
        rq, rk, rv, rz, mq, mk, mv, mz = jnp.split(proj, cuts, axis=-1)
        r_out = retention(rotary(heads(rq, RET_HEADS), pos), rotary(heads(rk, RET_HEADS), pos),
                          heads(rv, RET_HEADS), ret_norm_g[i])
        r_out = jax.nn.silu(rz.astype(f32)) * r_out
        q_m = rms_norm(heads(mq, MOBA_HEADS), q_norm_g[i])
        k_m = rms_norm(heads(mk, MOBA_HEADS), k_norm_g[i])
        m_out = moba(q_m, k_m, heads(mv, MOBA_HEADS), rel_bias)
        m_out = jax.nn.silu(mz.astype(f32)) * m_out
        y = jnp.concatenate([r_out, m_out], axis=-1).astype(x.dtype) @ w_out[i]
        x = x + y
        gate = jax.nn.sigmoid((rms_norm(x) @ w_ple_gate[i]).astype(f32))
        e = rms_norm(p[i] @ w_ple[i], ple_norm_g[i]).astype(f32)
        x = x + (gate * e).astype(x.dtype)
    return x
```

```python
import math
from contextlib import ExitStack

import numpy as np
import concourse.bass as bass
import concourse.mybir as mybir
from concourse.bass_utils import run_bass_kernel_spmd

F32 = mybir.dt.float32
BF16 = mybir.dt.bfloat16
AF = mybir.ActivationFunctionType
ALU = mybir.AluOpType
AX = mybir.AxisListType

S = 8192
D = 1024
G = 512
NG = S // G
NH = 2
L = 2
PLE = 256
NEG = -30000.0
NCORES = 8

ENGS = ("pe", "act", "dve", "pool", "sp")


class Buf:
    __slots__ = ("name", "w", "r", "dram", "key")

    def __init__(self, name, dram=False):
        self.name = name
        self.w = {}
        self.r = {}
        self.dram = dram
        self.key = ("dma", name)


class Eng:
    def __init__(self, name):
        self.name = name
        self.key = ("eng", name)
        self.count = 0
        self.waited = {}
        self.prog = []
        self.pending_noninc = False


class Tracker:
    def __init__(self, nc, stack):
        self.nc = nc
        self.stack = stack
        self.eng = {n: Eng(n) for n in ENGS}
        self.sems = {}
        self.dma_total = {}
        for n in ENGS:
            if n != "sp":
                self._sem(self.eng[n].key)
        self.nops = 0

    def _sem(self, key):
        if key not in self.sems:
            nm = "s_" + "_".join(str(k) for k in key)
            self.sems[key] = self.stack.enter_context(self.nc.semaphore(nm))
            self.dma_total.setdefault(key, 0)
        return self.sems[key]

    def _wait(self, E, toks):
        for k, v in toks.items():
            if E.waited.get(k, 0) >= v:
                continue
            E.waited[k] = v
            E.prog.append(("wait", k, v))

    def op(self, eng, fn, reads=(), writes=(), inc=True):
        E = self.eng[eng]
        toks = {}
        for b in list(reads) + list(writes):
            for k, v in b.w.items():
                if k == E.key and eng == "pe":
                    continue
                if toks.get(k, 0) < v:
                    toks[k] = v
        for b in writes:
            for k, v in b.r.items():
                if k == E.key and eng == "pe":
                    continue
                if toks.get(k, 0) < v:
                    toks[k] = v
        self._wait(E, toks)
        if inc:
            E.count += 1
            tok = (E.key, E.count)
            E.pending_noninc = False
        else:
            assert eng == "pe"
            tok = (E.key, E.count + 1)
            E.pending_noninc = True
        E.prog.append(("op", fn, inc))
        for b in reads:
            if b.r.get(tok[0], 0) < tok[1]:
                b.r[tok[0]] = tok[1]
        for b in writes:
            if b.dram:
                b.w[tok[0]] = tok[1]
            else:
                b.w = {tok[0]: tok[1]}
                b.r = {}
        self.nops += 1

    def dma(self, q, out, in_, reads=(), writes=(), sembuf=None):
        E = self.eng[q]
        key = sembuf.key
        self._sem(key)
        toks = {}
        for b in reads:
            for k, v in b.w.items():
                if toks.get(k, 0) < v:
                    toks[k] = v
        for b in writes:
            if not b.dram:
                for k, v in b.w.items():
                    if k == key:
                        continue
                    if toks.get(k, 0) < v:
                        toks[k] = v
            for k, v in b.r.items():
                if toks.get(k, 0) < v:
                    toks[k] = v
        self._wait(E, toks)
        self.dma_total[key] += 16
        tot = self.dma_total[key]
        E.prog.append(("dma", out, in_, key))
        for b in reads:
            if b.r.get(key, 0) < tot:
                b.r[key] = tot
        for b in writes:
            if b.dram:
                b.w[key] = tot
            else:
                b.w = {key: tot}
                b.r = {}
        self.nops += 1

    def custom(self, eng, fn, key, incv, reads=(), writes=()):
        E = self.eng[eng]
        self._sem(key)
        toks = {}
        for b in list(reads) + list(writes):
            for k, v in b.w.items():
                if toks.get(k, 0) < v:
                    toks[k] = v
        for b in writes:
            for k, v in b.r.items():
                if toks.get(k, 0) < v:
                    toks[k] = v
        self._wait(E, toks)
        self.dma_total[key] += incv
        tot = self.dma_total[key]
        E.prog.append(("custom", fn, key))
        for b in reads:
            b.r[key] = tot
        for b in writes:
            b.w[key] = tot

    def all_tokens(self):
        toks = {}
        for n, E in self.eng.items():
            if n != "sp" and E.count > 0:
                assert not E.pending_noninc, n
                toks[E.key] = E.count
        for k, v in self.dma_total.items():
            if k[0] != "eng" and v > 0:
                toks[k] = v
        return toks

    def emit_block(self):
        toks = self.all_tokens()
        self._wait(self.eng["sp"], {k: v for k, v in toks.items() if k[0] != "eng"})
        progs = {n: self.eng[n].prog for n in ENGS}
        sems = self.sems
        engsem = {n: sems.get(self.eng[n].key) for n in ENGS}

        def replay(e, n):
            my = engsem[n]
            for it in progs[n]:
                if it[0] == "wait":
                    e.wait_ge(sems[it[1]], it[2])
                elif it[0] == "op":
                    ins = it[1](e)
                    if it[2]:
                        ins.then_inc(my, 1)
                elif it[0] == "dma":
                    e.dma_start(out=it[1], in_=it[2]).then_inc(sems[it[3]], 16)
                elif it[0] == "custom":
                    it[1](e, sems[it[2]])

        with self.nc.Block() as block:
            @block.tensor
            def _(e):
                replay(e, "pe")

            @block.scalar
            def _(e):
                replay(e, "act")

            @block.vector
            def _(e):
                replay(e, "dve")

            @block.gpsimd
            def _(e):
                replay(e, "pool")

            @block.sync
            def _(e):
                replay(e, "sp")

        for n in ENGS:
            self.eng[n].prog = []
            for k, v in toks.items():
                if self.eng[n].waited.get(k, 0) < v:
                    self.eng[n].waited[k] = v


class TL:
    __slots__ = ("t", "b")

    def __init__(self, t, name):
        self.t = t
        self.b = Buf(name)


class Rot:
    def __init__(self, items):
        self.items = items
        self.i = 0

    def next(self):
        it = self.items[self.i % len(self.items)]
        self.i += 1
        return it


def _t5_bucket(n):
    n = np.asarray(n, dtype=np.int64)
    max_exact = 16
    nf = np.maximum(n, 1).astype(np.float32)
    large = max_exact + (np.log(nf / np.float32(max_exact)) / np.float32(math.log(2048 / max_exact))
                         * np.float32(32 - max_exact)).astype(np.int32)
    large = np.minimum(large, 31)
    return np.where(n < max_exact, n, large).astype(np.int64)


def _nfar():
    b = _t5_bucket(np.arange(0, 4096))
    nf = int(np.argmax(b == 31))
    assert np.all(b[nf:] == 31)
    return nf


NFAR = _nfar()
D0MAX = ((NFAR + 126) // 128) * 128
WT = D0MAX + 384 + 512


def _host_consts():
    c = {}
    half = 64
    inv = (10000.0 ** (-np.arange(half, dtype=np.float32) / half)).astype(np.float32)
    ang = np.arange(S, dtype=np.float32)[None, :] * inv[:, None]
    cos = np.cos(ang).astype(np.float32)
    sin = np.sin(ang).astype(np.float32)
    c["cosT"] = np.ascontiguousarray(np.concatenate([cos, cos], 0))
    c["sinT"] = np.ascontiguousarray(np.concatenate([-sin, sin], 0))
    c["ident"] = np.eye(128, dtype=np.float32)
    es = np.zeros((32, 32, 128), np.float32)
    for j in range(32):
        es[j, j, :] = 1.0
    c["esel"] = es.reshape(32, 32 * 128)
    i = np.arange(128)[:, None]
    t = np.arange(WT)[None, :]
    n = t - 384 - i
    c["cz"] = np.where(n >= 0, 0.0, NEG).astype(np.float32)
    c["bucket"] = _t5_bucket(np.maximum(n, 0))
    return c


def _ret_tables(heads):
    C = 128
    dect = np.zeros((128, NH, 512), np.float32)
    gpow = np.zeros((128, NH, 512), np.float32)
    kdec = np.zeros((128, NH), np.float32)
    gC = np.zeros((128, NH), np.float32)
    j = np.arange(C, dtype=np.float64)
    for hl, h in enumerate(heads):
        log_g = np.log1p(-np.power(2.0, -5.0 - h))
        rel = j[None, :] - j[:, None]
        dT = np.where(rel >= 0, np.exp(log_g * np.maximum(rel, 0.0)), 0.0) * (128 ** -0.5)
        dect[:, hl, :] = np.tile(dT, (1, 4))
        gpow[:, hl, :] = np.tile(np.exp(log_g * (j + 1.0))[None, :], (128, 4))
        kdec[:, hl] = np.exp(log_g * (C - 1 - j)) * (128 ** -0.5)
        gC[:, hl] = np.exp(log_g * C)
    return dect, gpow, kdec, gC


def build_program(debug=False, n_layers=L):
    import os
    NGR = int(os.environ.get("MK_NG", NG))
    STOP = int(os.environ.get("MK_STOP", 99))
    nc = bass.Bass("TRN2", target_bir_lowering=False)

    def din(name, shape):
        return nc.dram_tensor(name, shape, F32, kind="ExternalInput").ap()

    x_in = din("x", [S, D])
    p_in = din("p", [L, S, PLE])
    w_in = din("w_in", [L, D, 2048])
    w_out = din("w_out", [L, D, D])
    w_gate = din("w_gate", [L, D, D])
    w_ple = din("w_ple", [L, PLE, D])
    vecs_in = din("vecs", [128, 32])
    pgt_in = din("pgt", [128, L, D])
    cos_in = din("cosT", [128, S])
    sin_in = din("sinT", [128, S])
    dect_in = din("dect", [128, NH, 512])
    gpow_in = din("gpow", [128, NH, 512])
    wg_in = din("wg", [128, NH, WT])
    cz_in = din("cz", [128, WT])
    ident_in = din("ident", [128, 128])
    esel_in = din("esel", [32, 4096])
    y_out = nc.dram_tensor("y", [S, D], F32, kind="ExternalOutput").ap()

    xs_kind = "ExternalOutput" if debug else "Internal"
    xs = nc.dram_tensor("xs", [S, D], F32, kind=xs_kind).ap()
    dk = {"kind": "ExternalOutput"} if debug else {}
    QT = [nc.dram_tensor(f"QT{l}", [NH, 128, S], BF16, **(dk if l == 0 else {})).ap() for l in range(L)]
    KT = [nc.dram_tensor(f"KT{l}", [NH, 128, S], BF16, **(dk if l == 0 else {})).ap() for l in range(L)]
    ZT = [nc.dram_tensor(f"ZT{l}", [NH, 128, S], BF16).ap() for l in range(L)]
    VS = [nc.dram_tensor(f"VS{l}", [NH, S, 128], BF16).ap() for l in range(L)]
    KM = [nc.dram_tensor(f"KM{l}", [128, NH * 32], F32).ap() for l in range(L)]
    NCH = 4
    CW = S // NCH
    gl_t = [[nc.dram_tensor(f"gl{l}_{c}", [512, CW], BF16) for c in range(NCH)] for l in range(L)]
    gf_t = [[nc.dram_tensor(f"gf{l}_{c}", [1024, CW], BF16) for c in range(NCH)] for l in range(L)]

    def gl_ap(l, r0, r1, g):
        return gl_t[l][g // 4].ap()[r0:r1, (g % 4) * G:(g % 4 + 1) * G]

    def gf_ap(l, g):
        return gf_t[l][g // 4].ap()[:, (g % 4) * G:(g % 4 + 1) * G]
    d_gl = [Buf(f"d_gl{l}", dram=True) for l in range(L)]
    d_gf = [Buf(f"d_gf{l}", dram=True) for l in range(L)]
    d_scr = Buf("d_scr", dram=True)

    with ExitStack() as gst:
        T = Tracker(nc, gst)

        def OP(eng, meth, reads, writes, inc=True, **kw):
            T.op(eng, lambda e: getattr(e, meth)(**kw), reads, writes, inc)

        uid = [0]

        def sb(st, name, shape, dt):
            uid[0] += 1
            return TL(st.enter_context(nc.sbuf_tensor(f"sb_{name}_{uid[0]}", shape, dt)), name)

        def psum_banks(st):
            uid[0] += 1
            return [TL(st.enter_context(nc.psum_tensor(f"pb{i}_{uid[0]}", [128, 512], F32)), f"pb{i}")
                    for i in range(8)]

        vecs = sb(gst, "vecs", [128, 32], F32)
        ident_f = sb(gst, "ident_f", [128, 128], F32)
        ident = sb(gst, "ident_bf", [128, 128], BF16)
        ones = sb(gst, "ones_bf", [128, 128], BF16)
        onesd = sb(gst, "onesd_bf", [128, 128], BF16)
        eps = sb(gst, "eps", [128, 1], F32)
        gqs = sb(gst, "gqs", [128, L], F32)
        T.dma("sp", vecs.t[:], vecs_in[:, :], writes=[vecs.b], sembuf=vecs.b)
        T.dma("sp", ident_f.t[:], ident_in[:, :], writes=[ident_f.b], sembuf=ident_f.b)
        OP("dve", "tensor_copy", [ident_f.b], [ident.b], out=ident.t[:], in_=ident_f.t[:])
        OP("pool", "memset", [], [ones.b], ap=ones.t[:], constant=1.0)
        OP("pool", "memset", [], [onesd.b], ap=onesd.t[:], constant=1.0 / 128)
        OP("pool", "memset", [], [eps.b], ap=eps.t[:], constant=1e-6)
        OP("dve", "tensor_scalar", [vecs.b], [gqs.b], out=gqs.t[:], in0=vecs.t[:, 20:22],
           scalar1=float(128 ** -0.5), scalar2=None, op0=ALU.mult)
        T.emit_block()

        def phase1(l):
            x_src = x_in if l == 0 else xs
            with ExitStack() as st:
                pb = psum_banks(st)
                pj = Rot(pb[0:2])
                aux = Rot(pb[2:5])
                tr_slots = [pb[5], pb[6]]
                tr_ap = [pb[5].t[:, 0:256].bitcast(BF16), pb[6].t[:, 0:256].bitcast(BF16)]
                mb = pb[7]
                kT_ap = mb.t[:, 0:256].bitcast(BF16)
                kT_b = mb.b
                pkv_ap = mb.t[:, 256:384]
                pkv_b = mb.b
                psg_ap = mb.t[:, 384:416]
                psg_b = mb.b
                pmT_ap = mb.t[0:32, 416:480].bitcast(BF16)
                pmT_b = mb.b

                stg = [sb(st, f"stg{i}", [128, 1024], F32) for i in range(2)]
                xt = [sb(st, f"xt{i}", [128, 1024], F32) for i in range(3)]
                junk = sb(st, "junk", [128, 1024], BF16)
                ss = [sb(st, f"ss{i}", [128, 1], F32) for i in range(3)]
                hb = [sb(st, f"hb{i}", [128, 4, 1024], BF16) for i in range(2)]
                hT = [sb(st, f"hT{i}", [128, 8, 512], BF16) for i in range(2)]
                cs = [sb(st, f"cs{i}", [128, 2, 512], F32) for i in range(2)]
                dect = sb(st, "dect", [128, NH, 512], F32)
                gpow = sb(st, "gpow", [128, NH, 512], F32)
                t1 = sb(st, "t1", [128, 512], F32)
                t2 = sb(st, "t2", [128, 512], F32)
                QrT = [sb(st, f"QrT{i}", [128, 512], BF16) for i in range(2)]
                KrT = [sb(st, f"KrT{i}", [128, 512], BF16) for i in range(2)]
                SdT = [sb(st, f"SdT{i}", [128, 512], BF16) for i in range(2)]
                QdT = [sb(st, f"QdT{i}", [128, 512], BF16) for i in range(2)]
                Kd = [sb(st, f"Kd{i}", [128, 512], BF16) for i in range(2)]
                vsb = [sb(st, f"vsb{i}", [128, 4, 512], BF16) for i in range(2)]
                state = [sb(st, f"state{i}", [128, 128], F32) for i in range(NH)]
                stbf = [sb(st, f"stbf{i}", [128, 128], BF16) for i in range(NH)]
                o_bf = sb(st, "o_bf", [128, 512], BF16)
                osq = sb(st, "osq_bf", [128, 512], BF16)
                mean = sb(st, "mean_sb", [128, 512], F32)
                vv = sb(st, "vv", [128, 512], F32)
                cen = sb(st, "cen", [128, 512], F32)
                sz = [sb(st, f"sz{i}", [128, 512], BF16) for i in range(NH)]
                rout = [sb(st, f"rout{i}", [128, 512], BF16) for i in range(2)]
                mzs = [sb(st, f"mzs{i}", [128, 512], BF16) for i in range(2)]
                qsq = sb(st, "qsq", [128, 512], BF16)
                q32 = sb(st, "q32", [128, 512], F32)
                sdq = sb(st, "sdq", [128, 512], F32)
                qn32 = sb(st, "qn32", [128, 512], F32)
                kn32 = sb(st, "kn32", [128, 512], F32)
                qnb = [sb(st, f"qnb{i}", [128, 512], BF16) for i in range(2)]
                knb = [sb(st, f"knb{i}", [128, 512], BF16) for i in range(2)]
                km = sb(st, "km", [128, NH, 32], F32)
                kmh = sb(st, "kmh", [128, NH, 128], BF16)
                kmc = sb(st, "kmc", [128, 2], F32)
                kml = sb(st, "kml", [128, NH, 128], BF16)
                qlo = sb(st, "qlo", [128, 512], BF16)
                qgb = sb(st, "qgb", [128, 512], BF16)
                gm = sb(st, "gm", [128, 40], F32)
                m8 = sb(st, "m8", [128, 8], F32)
                selb = sb(st, "selb", [128, 32], BF16)
                maskT = [sb(st, f"maskT{i}", [32, 512], BF16) for i in range(2)]
                w_sb = sb(st, "w_in_sb", [128, 8, 2048], BF16)

                for kt in range(8):
                    for hf in range(2):
                        i = kt * 2 + hf
                        s_ = stg[i % 2]
                        T.dma("sp", s_.t[:], w_in[l, kt * 128:(kt + 1) * 128, hf * 1024:(hf + 1) * 1024],
                              writes=[s_.b], sembuf=s_.b)
                        OP("pool" if i % 2 == 0 else "dve", "tensor_copy", [s_.b], [w_sb.b],
                           out=w_sb.t[:, kt, hf * 1024:(hf + 1) * 1024], in_=s_.t[:])
                T.dma("sp", dect.t[:], dect_in[:, :, :], writes=[dect.b], sembuf=dect.b)
                T.dma("sp", gpow.t[:], gpow_in[:, :, :], writes=[gpow.b], sembuf=gpow.b)
                for hl in range(NH):
                    OP("pool", "memset", [], [state[hl].b], ap=state[hl].t[:], constant=0.0)
                    OP("pool", "memset", [], [stbf[hl].b], ap=stbf[hl].t[:], constant=0.0)
                OP("pool", "memset", [], [gm.b], ap=gm.t[:], constant=-1e30)
                OP("pool", "memset", [], [km.b], ap=km.t[:], constant=0.0)
                OP("pool", "memset", [], [kmh.b], ap=kmh.t[:], constant=0.0)
                OP("pool", "memset", [], [kml.b], ap=kml.t[:], constant=0.0)

                def load_x(ti):
                    if ti >= NG * 4:
                        return
                    xb = xt[ti % 3]
                    T.dma("sp", xb.t[:], x_src[ti * 128:(ti + 1) * 128, :], writes=[xb.b], sembuf=xb.b)

                def load_cs(g):
                    if g >= NG:
                        return
                    c_ = cs[g % 2]
                    T.dma("sp", c_.t[:, 0, :], cos_in[:, g * G:(g + 1) * G], writes=[c_.b], sembuf=c_.b)
                    T.dma("sp", c_.t[:, 1, :], sin_in[:, g * G:(g + 1) * G], writes=[c_.b], sembuf=c_.b)

                load_x(0)
                load_x(1)
                load_cs(0)

                evac_flip = [0]

                def proj_fm(ct, hTg):
                    bank = pj.next()
                    for kt in range(8):
                        OP("pe", "matmul", [w_sb.b, hTg.b], [bank.b], inc=(kt == 7),
                           out=bank.t[:], lhsT=w_sb.t[:, kt, ct * 128:(ct + 1) * 128], rhs=hTg.t[:, kt, :],
                           start=(kt == 0), stop=(kt == 7))
                    return bank

                def rotary(bank, c_, dst):
                    OP("dve", "tensor_tensor", [bank.b, c_.b], [t1.b], out=t1.t[:], in0=bank.t[:],
                       in1=c_.t[:, 0, :], op=ALU.mult)
                    OP("dve", "tensor_tensor", [bank.b, c_.b], [t2.b], out=t2.t[0:64, :], in0=bank.t[64:128, :],
                       in1=c_.t[0:64, 1, :], op=ALU.mult)
                    OP("dve", "tensor_tensor", [bank.b, c_.b], [t2.b], out=t2.t[64:128, :], in0=bank.t[0:64, :],
                       in1=c_.t[64:128, 1, :], op=ALU.mult)
                    OP("pool", "tensor_tensor", [t1.b, t2.b], [dst.b], out=dst.t[:], in0=t1.t[:], in1=t2.t[:],
                       op=ALU.add)

                for g in range(NGR):
                    par = g % 2
                    hbg, hTg, csg, vg = hb[par], hT[par], cs[par], vsb[par]
                    load_cs(g + 1)
                    for t in range(4):
                        ti = g * 4 + t
                        load_x(ti + 2)
                        xb, s1 = xt[ti % 3], ss[ti % 3]
                        OP("act", "activation", [xb.b], [junk.b, s1.b], out=junk.t[:], in_=xb.t[:],
                           func=AF.Square, accum_out=s1.t[:])
                        OP("act", "activation", [s1.b, eps.b], [s1.b], out=s1.t[:], in_=s1.t[:], func=AF.Sqrt,
                           scale=1.0 / D, bias=eps.t[:])
                        OP("dve", "reciprocal", [s1.b], [s1.b], out=s1.t[:], in_=s1.t[:])
                        OP("dve", "tensor_scalar", [xb.b, s1.b], [hbg.b], out=hbg.t[:, t, :], in0=xb.t[:],
                           scalar1=s1.t[:, 0:1], scalar2=None, op0=ALU.mult)
                    for kt in range(8):
                        sl = kt % 2
                        for t in range(4):
                            OP("pe", "transpose", [hbg.b, ident.b], [tr_slots[sl].b], inc=(t == 3),
                               out=tr_ap[sl][:, t * 128:(t + 1) * 128], in_=hbg.t[:, t, kt * 128:(kt + 1) * 128],
                               identity=ident.t[:])
                        OP("act", "activation", [tr_slots[sl].b, vecs.b], [hTg.b], out=hTg.t[:, kt, :],
                           in_=tr_ap[sl], func=AF.Identity, scale=vecs.t[:, l * 8 + kt:l * 8 + kt + 1])
                    for t in range(4 if STOP >= 2 else 0):
                        bank = pj.next()
                        for kt in range(8):
                            OP("pe", "matmul", [w_sb.b, hTg.b], [bank.b], inc=(kt == 7),
                               out=bank.t[:], lhsT=hTg.t[:, kt, t * 128:(t + 1) * 128], rhs=w_sb.t[:, kt, 1536:2048],
                               start=(kt == 0), stop=(kt == 7))
                        if t % 2 == 0:
                            OP("act", "copy", [bank.b], [vg.b], out=vg.t[:, t, :], in_=bank.t[:])
                        else:
                            OP("dve", "tensor_copy", [bank.b], [vg.b], out=vg.t[:, t, :], in_=bank.t[:])
                        T.dma("pool", VS[l][:, g * G + t * 128:g * G + (t + 1) * 128, :].rearrange("h p d -> p h d"),
                              vg.t[:, t, 256:512].rearrange("p (h d) -> p h d", h=NH),
                              reads=[vg.b], writes=[d_scr], sembuf=vg.b)
                    for hl in range(NH if STOP >= 3 else 0):
                        bank = proj_fm(4 + hl, hTg)
                        OP("act", "activation", [bank.b], [sz[hl].b], out=sz[hl].t[:], in_=bank.t[:], func=AF.Silu)
                    for hl in range(NH if STOP >= 4 else 0):
                        q_, k_, sd_, qd_, kd_ = QrT[hl], KrT[hl], SdT[hl], QdT[hl], Kd[hl]
                        bank = proj_fm(0 + hl, hTg)
                        rotary(bank, csg, q_)
                        bank = proj_fm(2 + hl, hTg)
                        rotary(bank, csg, k_)
                        bs = aux.next()
                        for c in range(4):
                            OP("pe", "matmul", [k_.b, q_.b], [bs.b], inc=(c == 3),
                               out=bs.t[:, c * 128:(c + 1) * 128], lhsT=k_.t[:, c * 128:(c + 1) * 128],
                               rhs=q_.t[:, c * 128:(c + 1) * 128], start=True, stop=True)
                        OP("dve", "tensor_tensor", [bs.b, dect.b], [sd_.b], out=sd_.t[:], in0=bs.t[:],
                           in1=dect.t[:, hl, :], op=ALU.mult)
                        OP("pool", "tensor_tensor", [q_.b, gpow.b], [qd_.b], out=qd_.t[:], in0=q_.t[:],
                           in1=gpow.t[:, hl, :], op=ALU.mult)
                        for c in range(4):
                            OP("pe", "transpose", [k_.b, ident.b], [kT_b], inc=(c == 3),
                               out=kT_ap[:, c * 128:(c + 1) * 128], in_=k_.t[:, c * 128:(c + 1) * 128],
                               identity=ident.t[:])
                        OP("act", "activation", [kT_b, vecs.b], [kd_.b], out=kd_.t[:], in_=kT_ap, func=AF.Identity,
                           scale=vecs.t[:, 24 + hl:25 + hl])
                        bo = aux.next()
                        for c in range(4):
                            vc = vg.t[:, c, hl * 128:(hl + 1) * 128]
                            OP("pe", "matmul", [vg.b, sd_.b], [bo.b], inc=False,
                               out=bo.t[:, c * 128:(c + 1) * 128], lhsT=vc, rhs=sd_.t[:, c * 128:(c + 1) * 128],
                               start=True, stop=False)
                            OP("pe", "matmul", [stbf[hl].b, qd_.b], [bo.b], inc=True,
                               out=bo.t[:, c * 128:(c + 1) * 128], lhsT=stbf[hl].t[:],
                               rhs=qd_.t[:, c * 128:(c + 1) * 128], start=False, stop=True)
                            OP("pe", "matmul", [kd_.b, vg.b], [pkv_b], inc=True,
                               out=pkv_ap, lhsT=kd_.t[:, c * 128:(c + 1) * 128], rhs=vc, start=True, stop=True)
                            OP("dve", "scalar_tensor_tensor", [state[hl].b, pkv_b, vecs.b], [state[hl].b],
                               out=state[hl].t[:], in0=state[hl].t[:], scalar=vecs.t[:, 26 + hl:27 + hl],
                               in1=pkv_ap, op0=ALU.mult, op1=ALU.add)
                            OP("pool", "tensor_copy", [state[hl].b], [stbf[hl].b], out=stbf[hl].t[:],
                               in_=state[hl].t[:])
                        OP("act", "copy", [bo.b], [o_bf.b], out=o_bf.t[:], in_=bo.t[:])
                        OP("act", "activation", [bo.b], [osq.b], out=osq.t[:], in_=bo.t[:], func=AF.Square)
                        bm = aux.next()
                        OP("pe", "matmul", [onesd.b, o_bf.b], [bm.b], out=bm.t[:], lhsT=onesd.t[:], rhs=o_bf.t[:],
                           start=True, stop=True)
                        bq = aux.next()
                        OP("pe", "matmul", [onesd.b, osq.b], [bq.b], out=bq.t[:], lhsT=onesd.t[:], rhs=osq.t[:],
                           start=True, stop=True)
                        OP("act", "copy", [bm.b], [mean.b], out=mean.t[:], in_=bm.t[:])
                        OP("dve", "tensor_tensor", [mean.b], [vv.b], out=vv.t[:], in0=mean.t[:], in1=mean.t[:],
                           op=ALU.mult)
                        OP("dve", "tensor_tensor", [bq.b, vv.b], [vv.b], out=vv.t[:], in0=bq.t[:], in1=vv.t[:],
                           op=ALU.subtract)
                        OP("act", "activation", [vv.b, eps.b], [vv.b], out=vv.t[:], in_=vv.t[:], func=AF.Sqrt,
                           bias=eps.t[:])
                        OP("dve", "reciprocal", [vv.b], [vv.b], out=vv.t[:], in_=vv.t[:])
                        OP("dve", "tensor_tensor", [bo.b, mean.b], [cen.b], out=cen.t[:], in0=bo.t[:],
                           in1=mean.t[:], op=ALU.subtract)
                        OP("pool", "tensor_tensor", [cen.b, vv.b], [cen.b], out=cen.t[:], in0=cen.t[:],
                           in1=vv.t[:], op=ALU.mult)
                        ro = rout[hl]
                        OP("dve", "scalar_tensor_tensor", [cen.b, vecs.b, sz[hl].b], [ro.b], out=ro.t[:],
                           in0=cen.t[:], scalar=vecs.t[:, 16 + l * 2 + hl:17 + l * 2 + hl], in1=sz[hl].t[:],
                           op0=ALU.mult, op1=ALU.mult)
                        T.dma("pool", gl_ap(l, hl * 128, (hl + 1) * 128, g), ro.t[:],
                              reads=[ro.b], writes=[d_gl[l]], sembuf=ro.b)
                    for hl in range(NH if STOP >= 5 else 0):
                        bank = proj_fm(10 + hl, hTg)
                        OP("act", "activation", [bank.b], [mzs[hl].b], out=mzs[hl].t[:], in_=bank.t[:], func=AF.Silu)
                        T.dma("pool", ZT[l][hl, :, g * G:(g + 1) * G], mzs[hl].t[:], reads=[mzs[hl].b],
                              writes=[d_scr], sembuf=mzs[hl].b)
                        for which in range(2):
                            ct = (8 if which == 0 else 6) + hl
                            dst32 = kn32 if which == 0 else qn32
                            dstb = (knb if which == 0 else qnb)[hl]
                            gcol = vecs.t[:, 22 + l:23 + l] if which == 0 else gqs.t[:, l:l + 1]
                            gbuf = vecs.b if which == 0 else gqs.b
                            bank = proj_fm(ct, hTg)
                            OP("act", "activation", [bank.b], [qsq.b], out=qsq.t[:], in_=bank.t[:], func=AF.Square)
                            OP("dve", "tensor_copy", [bank.b], [q32.b], out=q32.t[:], in_=bank.t[:])
                            bn = aux.next()
                            OP("pe", "matmul", [onesd.b, qsq.b], [bn.b], out=bn.t[:], lhsT=onesd.t[:],
                               rhs=qsq.t[:], start=True, stop=True)
                            OP("act", "activation", [bn.b, eps.b], [sdq.b], out=sdq.t[:], in_=bn.t[:],
                               func=AF.Sqrt, bias=eps.t[:])
                            OP("dve", "reciprocal", [sdq.b], [sdq.b], out=sdq.t[:], in_=sdq.t[:])
                            OP("dve", "scalar_tensor_tensor", [q32.b, gbuf, sdq.b], [dst32.b], out=dst32.t[:],
                               in0=q32.t[:], scalar=gcol, in1=sdq.t[:], op0=ALU.mult, op1=ALU.mult)
                            OP("act", "copy", [dst32.b], [dstb.b], out=dstb.t[:], in_=dst32.t[:])
                            if which == 0:
                                for b2 in range(2):
                                    OP("dve", "tensor_reduce", [dst32.b], [kmc.b],
                                       out=kmc.t[:, b2:b2 + 1],
                                       in_=dst32.t[:, b2 * 256:(b2 + 1) * 256], axis=AX.X, op=ALU.add)
                                if "k" not in os.environ.get("MK_SKIP", ""):
                                    OP("dve", "tensor_copy", [kmc.b], [km.b], out=km.t[:, hl, 2 * g:2 * g + 2],
                                       in_=kmc.t[:, 0:2])
                            dscr = KT if which == 0 else QT
                            T.dma("pool", dscr[l][hl, :, g * G:(g + 1) * G], dstb.t[:], reads=[dstb.b],
                                  writes=[d_scr], sembuf=dstb.b)
                        pass
                T.dma("pool", KM[l][:, :], km.t[:].rearrange("p h j -> p (h j)"), reads=[km.b], writes=[d_scr],
                      sembuf=km.b)
                T.emit_block()

        def phase2(l):
            with ExitStack() as st:
                pb = psum_banks(st)
                pss = Rot(pb[0:3])
                acco = [pb[3], pb[4]]
                accd = [pb[5], pb[6]]
                KTs = [sb(st, f"KTs{i}", [128, S], BF16) for i in range(2)]
                Vs = [sb(st, f"Vs{i}", [128, S // 128, 128], BF16) for i in range(2)]
                kmf = sb(st, "kmf", [128, NH, 32], F32)
                kmh = sb(st, "kmh2", [128, NH, 128], BF16)
                kml = sb(st, "kml2", [128, NH, 128], BF16)
                gm = sb(st, "gm2", [128, 40], F32)
                m8 = sb(st, "m8_2", [128, 8], F32)
                selb = sb(st, "selb2", [128, 32], BF16)
                mks = [sb(st, f"mks{i}", [32, G], BF16) for i in range(2)]
                esel = sb(st, "esel_bf", [32, 4096], BF16)
                eself = sb(st, "esel_f", [32, 4096], F32)
                wtab = sb(st, "wtab", [128, NH, WT], BF16)
                wgf = sb(st, "wgf", [128, WT], F32)
                czf = sb(st, "czf", [128, WT], F32)
                Qg = [sb(st, f"Qg{i}", [128, G], BF16) for i in range(2)]
                Zg = [sb(st, f"Zg{i}", [128, G], BF16) for i in range(2)]
                pT = Rot([sb(st, f"pT{i}", [128, G], BF16) for i in range(3)])
                rden = sb(st, "rden", [128, G], F32)
                tq = sb(st, "tq", [128, G], F32)
                mo = [sb(st, f"mo{i}", [128, G], BF16) for i in range(2)]

                T.dma("sp", eself.t[:], esel_in[:, :], writes=[eself.b], sembuf=eself.b)
                OP("pool", "tensor_copy", [eself.b], [esel.b], out=esel.t[:], in_=eself.t[:])
                T.dma("sp", czf.t[:], cz_in[:, :], writes=[czf.b], sembuf=czf.b)
                for hl in range(NH):
                    T.dma("sp", wgf.t[:], wg_in[:, hl, :], writes=[wgf.b], sembuf=wgf.b)
                    OP("dve", "tensor_scalar", [wgf.b, vecs.b], [wgf.b], out=wgf.t[:], in0=wgf.t[:],
                       scalar1=vecs.t[:, 28 + hl:29 + hl], scalar2=None, op0=ALU.subtract)
                    OP("dve", "tensor_tensor", [wgf.b, czf.b], [wtab.b], out=wtab.t[:, hl, :], in0=wgf.t[:],
                       in1=czf.t[:], op=ALU.add)

                def load_head(hl):
                    NT_ = NGR * G
                    T.dma("sp", KTs[hl].t[:, 0:NT_], KT[l][hl, :, 0:NT_], reads=[d_scr], writes=[KTs[hl].b],
                          sembuf=KTs[hl].b)
                    T.dma("sp", Vs[hl].t[:, 0:NT_ // 128, :],
                          VS[l][hl, 0:NT_, :].rearrange("(kt p) d -> p kt d", p=128),
                          reads=[d_scr], writes=[Vs[hl].b], sembuf=Vs[hl].b)

                qctr = [0, 0]

                def load_q(hl, g):
                    i = qctr[0] % 2
                    qctr[0] += 1
                    T.dma("sp", Qg[i].t[:], QT[l][hl, :, g * G:(g + 1) * G], reads=[d_scr], writes=[Qg[i].b],
                          sembuf=Qg[i].b)
                    T.dma("sp", Zg[i].t[:], ZT[l][hl, :, g * G:(g + 1) * G], reads=[d_scr], writes=[Zg[i].b],
                          sembuf=Zg[i].b)

                T.dma("sp", kmf.t[:].rearrange("p h j -> p (h j)"), KM[l][:, :], reads=[d_scr], writes=[kmf.b],
                      sembuf=kmf.b)
                OP("pool", "memset", [], [kmh.b], ap=kmh.t[:], constant=0.0)
                OP("pool", "memset", [], [kml.b], ap=kml.t[:], constant=0.0)
                OP("pool", "memset", [], [gm.b], ap=gm.t[:], constant=-1e30)
                for hl in range(NH):
                    OP("pool", "tensor_copy", [kmf.b], [kmh.b], out=kmh.t[:, hl, 0:32], in_=kmf.t[:, hl, :])
                    OP("pool", "tensor_tensor", [kmf.b, kmh.b], [kml.b], out=kml.t[:, hl, 0:32], in0=kmf.t[:, hl, :],
                       in1=kmh.t[:, hl, 0:32], op=ALU.subtract)
                selbank = pb[7]
                psg_ap = selbank.t[:, 0:128]
                pmT_ap = selbank.t[0:32, 128:192].bitcast(BF16)
                for hl in range(NH):
                    load_head(hl)
                load_q(0, 0)
                for hl in range(NH):
                    for g in range(NGR):
                        i = qctr[1] % 2
                        qctr[1] += 1
                        if g + 1 < NGR:
                            load_q(hl, g + 1)
                        elif hl + 1 < NH:
                            load_q(hl + 1, 0)
                        qg, zg = Qg[i], Zg[i]
                        ao, ad = acco[i], accd[i]
                        mk_ = mks[i]
                        for t in range(4):
                            qb = 2 * g + t // 2
                            OP("pool", "memset", [], [selb.b], ap=selb.t[:], constant=-1.0)
                            if qb > 0:
                                qs_ = slice(t * 128, (t + 1) * 128)
                                OP("pe", "matmul", [qg.b, kmh.b], [selbank.b], inc=False, out=psg_ap,
                                   lhsT=qg.t[:, qs_], rhs=kmh.t[:, hl, :], start=True, stop=False)
                                OP("pe", "matmul", [qg.b, kml.b], [selbank.b], inc=True, out=psg_ap,
                                   lhsT=qg.t[:, qs_], rhs=kml.t[:, hl, :], start=False, stop=True)
                                OP("dve", "tensor_copy", [selbank.b], [gm.b], out=gm.t[:, 8:8 + qb], in_=psg_ap[:, 0:qb])
                                OP("dve", "max", [gm.b], [m8.b], out=m8.t[:, 0:8], in_=gm.t[:, 0:8 + qb])
                                OP("dve", "tensor_scalar", [gm.b, m8.b], [selb.b], out=selb.t[:, 0:qb],
                                   in0=gm.t[:, 8:8 + qb], scalar1=m8.t[:, 2:3], scalar2=-1.0,
                                   op0=ALU.is_ge, op1=ALU.add)
                            OP("pool", "memset", [], [selb.b], ap=selb.t[:, qb:qb + 1], constant=0.0)
                            OP("pe", "transpose", [selb.b, ident.b], [selbank.b], out=pmT_ap, in_=selb.t[:],
                               identity=ident.t[:])
                            OP("act", "activation", [selbank.b], [mk_.b], out=mk_.t[:, t * 128:(t + 1) * 128],
                               in_=pmT_ap, func=AF.Identity, scale=-NEG)
                        nkt = 4 * g + 4
                        pend = []

                        def flush(item, last):
                            kt, p_ = item
                            OP("pe", "matmul", [Vs[hl].b, p_.b], [ao.b], inc=False, out=ao.t[:],
                               lhsT=Vs[hl].t[:, kt, :], rhs=p_.t[:], start=(kt == 0), stop=last)
                            OP("pe", "matmul", [ones.b, p_.b], [ad.b], inc=True, out=ad.t[:],
                               lhsT=ones.t[:], rhs=p_.t[:], start=(kt == 0), stop=last)

                        for kt in range(nkt):
                            d0 = g * G - kt * 128
                            near = d0 <= D0MAX
                            bank = pss.next()
                            OP("pe", "matmul", [KTs[hl].b, qg.b], [bank.b], inc=False, out=bank.t[:],
                               lhsT=KTs[hl].t[:, kt * 128:(kt + 1) * 128], rhs=qg.t[:], start=True, stop=False)
                            j = kt // 2
                            OP("pe", "matmul", [esel.b, mk_.b], [bank.b], inc=(not near), out=bank.t[:],
                               lhsT=esel.t[:, j * 128:(j + 1) * 128], rhs=mk_.t[:],
                               start=False, stop=(not near))
                            if near:
                                off = d0 + 384
                                OP("pe", "matmul", [ident.b, wtab.b], [bank.b], inc=True, out=bank.t[:],
                                   lhsT=ident.t[:], rhs=wtab.t[:, hl, off:off + G], start=False, stop=True)
                            p_ = pT.next()
                            OP("act", "activation", [bank.b], [p_.b], out=p_.t[:], in_=bank.t[:], func=AF.Exp)
                            pend.append((kt, p_))
                            if len(pend) > 1:
                                flush(pend.pop(0), False)
                        flush(pend.pop(0), True)
                        OP("dve", "reciprocal", [ad.b], [rden.b], out=rden.t[:], in_=ad.t[:])
                        OP("dve", "tensor_tensor", [ao.b, rden.b], [tq.b], out=tq.t[:], in0=ao.t[:], in1=rden.t[:],
                           op=ALU.mult)
                        m_ = mo[i]
                        OP("pool", "tensor_tensor", [tq.b, zg.b], [m_.b], out=m_.t[:], in0=tq.t[:], in1=zg.t[:],
                           op=ALU.mult)
                        T.dma("pool", gl_ap(l, 256 + hl * 128, 256 + (hl + 1) * 128, g), m_.t[:],
                              reads=[m_.b], writes=[d_gl[l]], sembuf=m_.b)

                for ch in range(NCH):
                    def cc(e, sem, ch=ch):
                        e.collective_compute("AllGather", ALU.bypass,
                                             replica_groups=[[0, 1], [2, 3], [4, 5], [6, 7]],
                                             ins=[gl_t[l][ch].ap().opt()],
                                             outs=[gf_t[l][ch].ap().opt()]).then_inc(sem, 1)
                    T.custom("pool", cc, ("cc", l, ch), 1, reads=[d_gl[l]], writes=[d_gf[l]])
                T.emit_block()

        def phase3(l):
            x_src = x_in if l == 0 else xs
            x_dst = xs if l < n_layers - 1 else y_out
            with ExitStack() as st:
                pb = psum_banks(st)
                wo = sb(st, "wo_sb", [128, 8, D], BF16)
                wgt = sb(st, "wgate_sb", [128, 8, D], BF16)
                wp = sb(st, "wple_sb", [128, 2, D], BF16)
                pgt = sb(st, "pgt", [128, D], F32)
                stg = [sb(st, f"stg{i}", [128, 1024], F32) for i in range(2)]
                gTg = [sb(st, f"gTg{i}", [128, 8, G], BF16) for i in range(2)]
                xg = [sb(st, f"xg{i}", [128, 4, D], F32) for i in range(2)]
                pg_ = [sb(st, f"pin{i}", [128, 4, PLE], F32) for i in range(2)]
                x1 = [sb(st, f"x1_{i}", [128, D], F32) for i in range(2)]
                junk = sb(st, "junk", [128, D], BF16)
                s1 = [sb(st, f"s1_{i}", [128, 1], F32) for i in range(2)]
                s2 = [sb(st, f"s2_{i}", [128, 1], F32) for i in range(2)]
                h1 = [sb(st, f"h1_{i}", [128, D], BF16) for i in range(2)]
                h1T = [sb(st, f"h1T{i}", [128, 8, 128], BF16) for i in range(2)]
                gate = [sb(st, f"gate{i}", [128, D], F32) for i in range(2)]
                ee = [sb(st, f"ee{i}", [128, D], F32) for i in range(2)]
                pbf = [sb(st, f"pbf{i}", [128, PLE], BF16) for i in range(2)]
                pTs = [sb(st, f"pTs{i}", [128, 2, 128], BF16) for i in range(2)]
                x2 = [sb(st, f"x2_{i}", [128, D], F32) for i in range(2)]

                i = 0
                for (src, dst, nk) in ((w_out, wo, 8), (w_gate, wgt, 8), (w_ple, wp, 2)):
                    for kt in range(nk):
                        s_ = stg[i % 2]
                        T.dma("sp", s_.t[:], src[l, kt * 128:(kt + 1) * 128, :], writes=[s_.b], sembuf=s_.b)
                        OP("pool" if i % 2 == 0 else "dve", "tensor_copy", [s_.b], [dst.b], out=dst.t[:, kt, :],
                           in_=s_.t[:])
                        i += 1
                T.dma("sp", pgt.t[:], pgt_in[:, l, :], writes=[pgt.b], sembuf=pgt.b)

                def load_g(g):
                    if g >= NGR:
                        return
                    i = g % 2
                    T.dma("sp", gTg[i].t[:], gf_ap(l, g).rearrange("(kt p) n -> p kt n", p=128),
                          reads=[d_gf[l]], writes=[gTg[i].b], sembuf=gTg[i].b)
                    T.dma("sp", xg[i].t[:], x_src[g * G:(g + 1) * G, :].rearrange("(t p) d -> p t d", p=128),
                          reads=[d_scr], writes=[xg[i].b], sembuf=xg[i].b)
                    T.dma("sp", pg_[i].t[:], p_in[l, g * G:(g + 1) * G, :].rearrange("(t p) d -> p t d", p=128),
                          writes=[pg_[i].b], sembuf=pg_[i].b)

                py = [pb[0], pb[1]]
                pgb = [pb[2], pb[3]]
                peb = [pb[4], pb[5]]
                trb = pb[6]
                tr_ap = trb.t[:].bitcast(BF16)
                ptb = pb[7]
                pt_ap = ptb.t[:, 0:128].bitcast(BF16)
                load_g(0)
                for g in range(NGR):
                    load_g(g + 1)
                    gi = g % 2
                    for t in range(4):
                        k = (g * 4 + t) % 2
                        for hf in range(2):
                            for kt in range(8):
                                OP("pe", "matmul", [gTg[gi].b, wo.b], [py[hf].b], inc=(kt == 7), out=py[hf].t[:],
                                   lhsT=gTg[gi].t[:, kt, t * 128:(t + 1) * 128],
                                   rhs=wo.t[:, kt, hf * 512:(hf + 1) * 512], start=(kt == 0), stop=(kt == 7))
                            OP("dve", "tensor_tensor", [py[hf].b, xg[gi].b], [x1[k].b],
                               out=x1[k].t[:, hf * 512:(hf + 1) * 512], in0=py[hf].t[:],
                               in1=xg[gi].t[:, t, hf * 512:(hf + 1) * 512], op=ALU.add)
                        OP("act", "activation", [x1[k].b], [junk.b, s1[k].b], out=junk.t[:], in_=x1[k].t[:],
                           func=AF.Square, accum_out=s1[k].t[:])
                        OP("act", "activation", [s1[k].b, eps.b], [s1[k].b], out=s1[k].t[:], in_=s1[k].t[:],
                           func=AF.Sqrt, scale=1.0 / D, bias=eps.t[:])
                        OP("dve", "reciprocal", [s1[k].b], [s1[k].b], out=s1[k].t[:], in_=s1[k].t[:])
                        OP("dve", "tensor_scalar", [x1[k].b, s1[k].b], [h1[k].b], out=h1[k].t[:], in0=x1[k].t[:],
                           scalar1=s1[k].t[:, 0:1], scalar2=None, op0=ALU.mult)
                        for kt in range(8):
                            OP("pe", "transpose", [h1[k].b, ident.b], [trb.b], inc=(kt == 7),
                               out=tr_ap[:, kt * 128:(kt + 1) * 128], in_=h1[k].t[:, kt * 128:(kt + 1) * 128],
                               identity=ident.t[:])
                        OP("act", "copy", [trb.b], [h1T[k].b], out=h1T[k].t[:].rearrange("p a b -> p (a b)"),
                           in_=tr_ap)
                        OP("pool", "tensor_copy", [pg_[gi].b], [pbf[k].b], out=pbf[k].t[:], in_=pg_[gi].t[:, t, :])
                        for j in range(2):
                            OP("pe", "transpose", [pbf[k].b, ident.b], [ptb.b], inc=(j == 1),
                               out=pt_ap[:, j * 128:(j + 1) * 128], in_=pbf[k].t[:, j * 128:(j + 1) * 128],
                               identity=ident.t[:])
                        OP("dve", "tensor_copy", [ptb.b], [pTs[k].b], out=pTs[k].t[:].rearrange("p a b -> p (a b)"),
                           in_=pt_ap)
                        for hf in range(2):
                            for kt in range(8):
                                OP("pe", "matmul", [h1T[k].b, wgt.b], [pgb[hf].b], inc=(kt == 7), out=pgb[hf].t[:],
                                   lhsT=h1T[k].t[:, kt, :], rhs=wgt.t[:, kt, hf * 512:(hf + 1) * 512],
                                   start=(kt == 0), stop=(kt == 7))
                            OP("act", "activation", [pgb[hf].b], [gate[k].b],
                               out=gate[k].t[:, hf * 512:(hf + 1) * 512], in_=pgb[hf].t[:], func=AF.Sigmoid)
                        for hf in range(2):
                            for j in range(2):
                                OP("pe", "matmul", [pTs[k].b, wp.b], [peb[hf].b], inc=(j == 1), out=peb[hf].t[:],
                                   lhsT=pTs[k].t[:, j, :], rhs=wp.t[:, j, hf * 512:(hf + 1) * 512],
                                   start=(j == 0), stop=(j == 1))
                            OP("act", "activation", [peb[hf].b], [junk.b, s2[k].b] if hf == 0 else [junk.b, s1[k].b],
                               out=junk.t[:, 0:512], in_=peb[hf].t[:], func=AF.Square,
                               accum_out=(s2[k].t[:] if hf == 0 else s1[k].t[:]))
                        OP("dve", "tensor_tensor", [s1[k].b, s2[k].b], [s2[k].b], out=s2[k].t[:], in0=s1[k].t[:],
                           in1=s2[k].t[:], op=ALU.add)
                        OP("act", "activation", [s2[k].b, eps.b], [s2[k].b], out=s2[k].t[:], in_=s2[k].t[:],
                           func=AF.Sqrt, scale=1.0 / D, bias=eps.t[:])
                        OP("dve", "reciprocal", [s2[k].b], [s2[k].b], out=s2[k].t[:], in_=s2[k].t[:])
                        for hf in range(2):
                            OP("dve", "scalar_tensor_tensor", [peb[hf].b, s2[k].b, pgt.b], [ee[k].b],
                               out=ee[k].t[:, hf * 512:(hf + 1) * 512], in0=peb[hf].t[:], scalar=s2[k].t[:, 0:1],
                               in1=pgt.t[:, hf * 512:(hf + 1) * 512], op0=ALU.mult, op1=ALU.mult)
                        OP("pool", "tensor_tensor", [gate[k].b, ee[k].b], [ee[k].b], out=ee[k].t[:],
                           in0=gate[k].t[:], in1=ee[k].t[:], op=ALU.mult)
                        OP("pool", "tensor_tensor", [x1[k].b, ee[k].b], [x2[k].b], out=x2[k].t[:], in0=x1[k].t[:],
                           in1=ee[k].t[:], op=ALU.add)
                        tok0 = g * G + t * 128
                        T.dma("pool", x_dst[tok0:tok0 + 128, :], x2[k].t[:], reads=[x2[k].b], writes=[d_scr],
                              sembuf=x2[k].b)
                T.emit_block()

        import os
        phs = os.environ.get("MK_PHASES", "123")
        for l in range(n_layers):
            if "1" in phs:
                phase1(l)
            if "2" in phs:
                phase2(l)
            if "3" in phs:
                phase3(l)
    return nc


def _prep_inputs(x, p, norm_g, w_in, ret_norm_g, q_norm_g, k_norm_g, rel_bias, w_out, w_ple, ple_norm_g,
                 w_ple_gate):
    f = np.float32
    x = np.asarray(x, f)
    p = np.asarray(p, f)
    w_in = np.asarray(w_in, f)
    w_out = np.asarray(w_out, f)
    w_ple = np.asarray(w_ple, f)
    w_gate = np.asarray(w_ple_gate, f)
    norm_g = np.asarray(norm_g, f)
    ret_norm_g = np.asarray(ret_norm_g, f)
    q_norm_g = np.asarray(q_norm_g, f)
    k_norm_g = np.asarray(k_norm_g, f)
    rel_bias = np.asarray(rel_bias, f)
    ple_norm_g = np.asarray(ple_norm_g, f)
    hc = _host_consts()
    perm = []
    for k in range(2):
        perm += list(range(2 * k * 128, 2 * k * 128 + 256))
        perm += list(range(512 + 2 * k * 128, 512 + 2 * k * 128 + 256))
    perm = np.array(perm)
    w_out_p = np.ascontiguousarray(w_out[:, perm, :])
    pgt = np.ascontiguousarray(np.broadcast_to(ple_norm_g.transpose(0, 1)[None, :, :], (128, L, D)))
    in_maps = []
    for c in range(NCORES):
        b, hs = c // 2, c % 2
        heads = [2 * hs, 2 * hs + 1]
        seg = lambda s: list(range(s * 512 + hs * 256, s * 512 + hs * 256 + 256))
        cols = seg(0) + seg(1) + seg(3) + seg(4) + seg(5) + seg(7) + seg(2) + seg(6)
        w_in_c = np.ascontiguousarray(w_in[:, :, cols])
        vecs = np.zeros((128, 32), f)
        for l in range(L):
            vecs[:, l * 8:(l + 1) * 8] = norm_g[l].reshape(8, 128).T
            for hl, h in enumerate(heads):
                vecs[:, 16 + l * 2 + hl] = ret_norm_g[l, h * 128:(h + 1) * 128]
            vecs[:, 20 + l] = q_norm_g[l]
            vecs[:, 22 + l] = k_norm_g[l]
        dect, gpow, kdec, gC = _ret_tables(heads)
        vecs[:, 24:26] = kdec
        vecs[:, 26:28] = gC
        for hl, h in enumerate(heads):
            vecs[:, 28 + hl] = rel_bias[31, h]
        wg = np.stack([rel_bias[:, h][hc["bucket"]] for h in heads], axis=1).astype(f)
        in_maps.append({
            "x": np.ascontiguousarray(x[b]),
            "p": np.ascontiguousarray(p[:, b]),
            "w_in": w_in_c, "w_out": w_out_p, "w_gate": w_gate, "w_ple": w_ple,
            "vecs": vecs, "pgt": pgt, "cosT": hc["cosT"], "sinT": hc["sinT"],
            "dect": dect, "gpow": gpow, "wg": np.ascontiguousarray(wg), "cz": hc["cz"],
            "ident": hc["ident"], "esel": hc["esel"],
        })
    return in_maps


def kernel(x, p, norm_g, w_in, ret_norm_g, q_norm_g, k_norm_g, rel_bias, w_out, w_ple, ple_norm_g, w_ple_gate):
    in_maps = _prep_inputs(x, p, norm_g, w_in, ret_norm_g, q_norm_g, k_norm_g, rel_bias, w_out, w_ple,
                           ple_norm_g, w_ple_gate)
    nc = build_program()
    res = run_bass_kernel_spmd(nc, in_maps, core_ids=list(range(NCORES)))
    out = np.stack([np.asarray(res.results[2 * b]["y"], np.float32) for b in range(4)], axis=0)
    return out
```
